# Optimizing a Trainium2 kernel written in Bass

```python
import math
import jax, jax.numpy as jnp
from jax import lax
import numpy as np

D_MODEL = 1024
BATCH = 8
SEQ = 8192
DEPTH = 1
DEC_BATCH = 1
DEC_SEQ = 16384
PAST_LEN = 128

N_MEM = 256
D_FF = 2816
CONV_WIDTH = 512
CONV_K = 31
SSM_WIDTH = 512
SSM_GROUP = 16
SSM_GROUPS = SSM_WIDTH // SSM_GROUP
SSM_STATE = 64
N_BRANCH = 2
IN_COLS = 2 * CONV_WIDTH + SSM_WIDTH + N_BRANCH * D_MODEL
X_HEADS = 4
X_HEAD_DIM = D_MODEL // X_HEADS
EPS = 1e-6
DT_MIN = 1e-3
DT_MAX = 1e-1

kernel_name = 'conformer_s5_gated_encoder'


def rmsnorm(x, g):
    xf = x.astype(jnp.float32)
    y = xf * lax.rsqrt(jnp.mean(xf * xf, axis=-1, keepdims=True) + EPS)
    return (y * g.astype(jnp.float32)).astype(x.dtype)


def layernorm(x, g, b):
    xf = x.astype(jnp.float32)
    xc = xf - jnp.mean(xf, axis=-1, keepdims=True)
    y = xc * lax.rsqrt(jnp.mean(xc * xc, axis=-1, keepdims=True) + EPS)
    return (y * g.astype(jnp.float32) + b.astype(jnp.float32)).astype(x.dtype)


def swiglu_ffn(x, w_gu, w_down):
    gate, up = jnp.split(x @ w_gu, 2, axis=-1)
    return (jax.nn.silu(gate) * up) @ w_down


def conv_module(z, w_dw, b_dw, ln_g, ln_b, w_pw):
    v, gate = jnp.split(z, 2, axis=-1)
    v = v * jax.nn.sigmoid(gate)
    pad = CONV_K // 2
    v = lax.conv_general_dilated(
        v, w_dw[:, None, :].astype(v.dtype), window_strides=(1,),
        padding=[(pad, pad)], dimension_numbers=('NWC', 'WIO', 'NWC'),
        feature_group_count=CONV_WIDTH) + b_dw
    v = jax.nn.silu(layernorm(v, ln_g, ln_b))
    return v @ w_pw


def _complex(re, im):
    return lax.complex(re.astype(jnp.float32), im.astype(jnp.float32))


def zoh_discretize(lam_re, lam_im, log_dt, b_re, b_im):
    lam = _complex(lam_re, lam_im)
    dt = jnp.exp(log_dt.astype(jnp.float32))[:, None]
    lam_bar = jnp.exp(lam * dt)
    b_bar = ((lam_bar - 1.0) / lam)[..., None] * _complex(b_re, b_im)
    return lam_bar, b_bar


def _linear_recurrence(e1, e2):
    a1, b1 = e1
    a2, b2 = e2
    return a1 * a2, a2 * b1 + b2


def diag_scan(u, lam_bar, b_bar, c, reverse):
    bu = jnp.einsum('lgh,gnh->lgn', u, b_bar)
    a = jnp.broadcast_to(lam_bar, bu.shape)
    _, s = lax.associative_scan(_linear_recurrence, (a, bu), reverse=reverse, axis=0)
    return jnp.einsum('lgn,ghn->lgh', s, c).real


def s5_branch(u, lam_re, lam_im, log_dt, b_re, b_im, c_re, c_im, d_skip, w_glu):
    lam_f, b_f = zoh_discretize(lam_re[0], lam_im[0], log_dt[0], b_re[0], b_im[0])
    lam_b, b_b = zoh_discretize(lam_re[1], lam_im[1], log_dt[1], b_re[1], b_im[1])
    c_f = _complex(c_re[0], c_im[0])
    c_b = _complex(c_re[1], c_im[1])
    d = d_skip.astype(jnp.float32).reshape(SSM_GROUPS, SSM_GROUP)

    def one_sequence(us):
        length = us.shape[0]
        ug = us.astype(jnp.float32).reshape(length, SSM_GROUPS, SSM_GROUP)
        uc = ug.astype(jnp.complex64)
        y = (diag_scan(uc, lam_f, b_f, c_f, False)
             + diag_scan(uc, lam_b, b_b, c_b, True)
             + d * ug)
        return y.reshape(length, SSM_WIDTH)

    y = lax.map(one_sequence, u).astype(u.dtype)
    a, g = jnp.split(jax.nn.gelu(y) @ w_glu, 2, axis=-1)
    return a * jax.nn.sigmoid(g)


def memory_cross_attention(q_in, mem, mem_g, w_q, w_kv, w_o):
    bsz, length, _ = q_in.shape
    m = rmsnorm(mem, mem_g)
    q = (q_in @ w_q).reshape(bsz, length, X_HEADS, X_HEAD_DIM)
    k, v = jnp.split(m @ w_kv, 2, axis=-1)
    k = k.reshape(bsz, N_MEM, X_HEADS, X_HEAD_DIM)
    v = v.reshape(bsz, N_MEM, X_HEADS, X_HEAD_DIM)
    s = jnp.einsum('bqhd,bkhd->bhqk', q, k).astype(jnp.float32) * (X_HEAD_DIM ** -0.5)
    p = jax.nn.softmax(s, axis=-1).astype(v.dtype)
    o = jnp.einsum('bhqk,bkhd->bqhd', p, v).reshape(bsz, length, D_MODEL)
    return o @ w_o


def encoder_layer(h, mem, p):
    h = h + 0.5 * swiglu_ffn(rmsnorm(h, p['ffn1_g']), p['ffn1_wgu'], p['ffn1_wd'])
    u = rmsnorm(h, p['mix_g'])
    z = u @ p['w_in'] + p['b_in']
    z_conv = z[..., :2 * CONV_WIDTH]
    z_ssm = z[..., 2 * CONV_WIDTH:2 * CONV_WIDTH + SSM_WIDTH]
    z_gate = z[..., 2 * CONV_WIDTH + SSM_WIDTH:]
    conv_out = conv_module(z_conv, p['conv_w'], p['conv_b'], p['conv_ln_g'], p['conv_ln_b'],
                           p['conv_w_pw'])
    ssm_out = s5_branch(z_ssm, p['ssm_lam_re'], p['ssm_lam_im'], p['ssm_log_dt'],
                        p['ssm_b_re'], p['ssm_b_im'], p['ssm_c_re'], p['ssm_c_im'],
                        p['ssm_d'], p['ssm_w_glu'])
    g_conv, g_ssm = jnp.split(jax.nn.sigmoid(z_gate), 2, axis=-1)
    h = h + (g_conv * conv_out + g_ssm * ssm_out) @ p['w_out']
    h = h + memory_cross_attention(rmsnorm(h, p['xattn_g']), mem, p['mem_g'],
                                   p['xattn_wq'], p['xattn_wkv'], p['xattn_wo'])
    h = h + 0.5 * swiglu_ffn(rmsnorm(h, p['ffn2_g']), p['ffn2_wgu'], p['ffn2_wd'])
    return h


def run_trunk(x, mem, layer_params, final_g):
    h = x
    for l in range(DEPTH):
        p = {name: arr[l] for name, arr in layer_params.items()}
        h = encoder_layer(h, mem, p)
    return rmsnorm(h, final_g)


def setup_inputs(seed: int = 0) -> dict:
    key = jax.random.key(seed)
    keys = iter(jax.random.split(key, 48))
    f32 = jnp.float32

    def nrm(shape, scale):
        return scale * jax.random.normal(next(keys), shape, f32)

    def gain(shape):
        return 1.0 + 0.02 * jax.random.normal(next(keys), shape, f32)

    G, N, H = SSM_GROUPS, SSM_STATE, SSM_GROUP
    lam_re = -0.5 + 0.01 * jax.random.normal(next(keys), (DEPTH, 2, G, N), f32)
    lam_im = (math.pi * jnp.arange(N, dtype=f32))[None, None, None, :] \
        + 0.01 * jax.random.normal(next(keys), (DEPTH, 2, G, N), f32)
    log_dt = jax.random.uniform(next(keys), (DEPTH, 2, G), f32,
                                minval=math.log(DT_MIN), maxval=math.log(DT_MAX))
    return {
        'x_prompt': nrm((BATCH, SEQ, D_MODEL), 1.0),
        'x_sample': nrm((DEC_BATCH, DEC_SEQ, D_MODEL), 1.0),
        'mem_prompt': nrm((BATCH, N_MEM, D_MODEL), 1.0),
        'mem_sample': nrm((DEC_BATCH, N_MEM, D_MODEL), 1.0),
        'ffn1_g': gain((DEPTH, D_MODEL)),
        'ffn1_wgu': nrm((DEPTH, D_MODEL, 2 * D_FF), D_MODEL ** -0.5),
        'ffn1_wd': nrm((DEPTH, D_FF, D_MODEL), D_FF ** -0.5),
        'mix_g': gain((DEPTH, D_MODEL)),
        'w_in': nrm((DEPTH, D_MODEL, IN_COLS), D_MODEL ** -0.5),
        'b_in': nrm((DEPTH, IN_COLS), 0.02),
        'conv_w': nrm((DEPTH, CONV_K, CONV_WIDTH), CONV_K ** -0.5),
        'conv_b': nrm((DEPTH, CONV_WIDTH), 0.02),
        'conv_ln_g': gain((DEPTH, CONV_WIDTH)),
        'conv_ln_b': nrm((DEPTH, CONV_WIDTH), 0.02),
        'conv_w_pw': nrm((DEPTH, CONV_WIDTH, D_MODEL), CONV_WIDTH ** -0.5),
        'ssm_lam_re': lam_re,
        'ssm_lam_im': lam_im,
        'ssm_log_dt': log_dt,
        'ssm_b_re': nrm((DEPTH, 2, G, N, H), (2.0 * H) ** -0.5),
        'ssm_b_im': nrm((DEPTH, 2, G, N, H), (2.0 * H) ** -0.5),
        'ssm_c_re': nrm((DEPTH, 2, G, H, N), (2.0 * N) ** -0.5),
        'ssm_c_im': nrm((DEPTH, 2, G, H, N), (2.0 * N) ** -0.5),
        'ssm_d': nrm((DEPTH, SSM_WIDTH), 1.0),
        'ssm_w_glu': nrm((DEPTH, SSM_WIDTH, 2 * D_MODEL), SSM_WIDTH ** -0.5),
        'w_out': nrm((DEPTH, D_MODEL, D_MODEL), D_MODEL ** -0.5),
        'xattn_g': gain((DEPTH, D_MODEL)),
        'mem_g': gain((DEPTH, D_MODEL)),
        'xattn_wq': nrm((DEPTH, D_MODEL, D_MODEL), D_MODEL ** -0.5),
        'xattn_wkv': nrm((DEPTH, D_MODEL, 2 * D_MODEL), D_MODEL ** -0.5),
        'xattn_wo': nrm((DEPTH, D_MODEL, D_MODEL), D_MODEL ** -0.5),
        'ffn2_g': gain((DEPTH, D_MODEL)),
        'ffn2_wgu': nrm((DEPTH, D_MODEL, 2 * D_FF), D_MODEL ** -0.5),
        'ffn2_wd': nrm((DEPTH, D_FF, D_MODEL), D_FF ** -0.5),
        'final_g': gain((D_MODEL,)),
    }


def reference(x_prompt, x_sample, mem_prompt, mem_sample,
              ffn1_g, ffn1_wgu, ffn1_wd,
              mix_g, w_in, b_in,
              conv_w, conv_b, conv_ln_g, conv_ln_b, conv_w_pw,
              ssm_lam_re, ssm_lam_im, ssm_log_dt, ssm_b_re, ssm_b_im, ssm_c_re, ssm_c_im,
              ssm_d, ssm_w_glu,
              w_out,
              xattn_g, mem_g, xattn_wq, xattn_wkv, xattn_wo,
              ffn2_g, ffn2_wgu, ffn2_wd,
              final_g):
    layer_params = dict(
        ffn1_g=ffn1_g, ffn1_wgu=ffn1_wgu, ffn1_wd=ffn1_wd,
        mix_g=mix_g, w_in=w_in, b_in=b_in,
        conv_w=conv_w, conv_b=conv_b, conv_ln_g=conv_ln_g, conv_ln_b=conv_ln_b,
        conv_w_pw=conv_w_pw,
        ssm_lam_re=ssm_lam_re, ssm_lam_im=ssm_lam_im, ssm_log_dt=ssm_log_dt,
        ssm_b_re=ssm_b_re, ssm_b_im=ssm_b_im, ssm_c_re=ssm_c_re, ssm_c_im=ssm_c_im,
        ssm_d=ssm_d, ssm_w_glu=ssm_w_glu,
        w_out=w_out,
        xattn_g=xattn_g, mem_g=mem_g, xattn_wq=xattn_wq, xattn_wkv=xattn_wkv,
        xattn_wo=xattn_wo,
        ffn2_g=ffn2_g, ffn2_wgu=ffn2_wgu, ffn2_wd=ffn2_wd,
    )
    y_prompt = run_trunk(x_prompt, mem_prompt, layer_params, final_g)
    y_sample = run_trunk(x_sample, mem_sample, layer_params, final_g)
    return (y_prompt, y_sample)
```

```python
import math
from contextlib import ExitStack

import numpy as np
import concourse.bass as bass
import concourse.mybir as mybir
from concourse.bass_utils import run_bass_kernel_spmd

F32 = mybir.dt.float32
BF16 = mybir.dt.bfloat16
I32 = mybir.dt.int32
AF = mybir.ActivationFunctionType
ALU = mybir.AluOpType

D = 1024
DFF = 2816
NF = 22
KT = 8
TT = 256
NS = TT // 128
KS = TT // 8
G = 32
NCORE = 8
EPS = 1e-6
HALO = 128
TWO_PI = 2.0 * math.pi
import os as _os
CUT = int(_os.environ.get('P3CUT', '0'))

ENGS = ("pe", "act", "dve", "pool", "sp")


class _Op:
    __slots__ = ("eng", "fn", "deps", "needs_inc", "inc_val", "dma_key", "dma_val", "is_dma", "dinc")

    def __init__(self, eng, fn, is_dma, dma_key, dinc):
        self.eng = eng
        self.fn = fn
        self.deps = []
        self.needs_inc = False
        self.inc_val = None
        self.is_dma = is_dma
        self.dma_key = dma_key
        self.dma_val = None
        self.dinc = dinc


class Sched:
    def __init__(self, nc, stack, n_dma_sems=56):
        self.nc = nc
        self.sems = {e: stack.enter_context(nc.semaphore("s_" + e)) for e in ENGS}
        self.bar = stack.enter_context(nc.semaphore("bar"))
        self.pool = [stack.enter_context(nc.semaphore("d%d" % i)) for i in range(n_dma_sems)]
        self.dma_sem_of = {}
        self.dma_count = {}
        self.eng_count = {e: 0 for e in ENGS}
        self.bar_count = 0
        self.ops = []
        self.buf = {}
        self.final_keys = []

    def _st(self, k):
        s = self.buf.get(k)
        if s is None:
            s = [None, []]
            self.buf[k] = s
        return s

    def add(self, eng, fn, reads=(), writes=(), dma_key=None, dinc=16):
        is_dma = dma_key is not None
        op = _Op(eng, fn, is_dma, dma_key, dinc)
        deps = {}
        for k in reads:
            st = self._st(k)
            if st[0] is not None:
                deps[id(st[0])] = (st[0], "RAW")
        for k in writes:
            st = self._st(k)
            if st[0] is not None and id(st[0]) not in deps:
                deps[id(st[0])] = (st[0], "WAW")
            for r in st[1]:
                if id(r) not in deps:
                    deps[id(r)] = (r, "WAR")
        dl = []
        for d, kind in deps.values():
            dv = self.dma_count[d.dma_key] if d.is_dma else None
            dl.append((d, kind, dv))
        for k in reads:
            st = self._st(k)
            if not is_dma and eng in ("pe", "act", "dve"):
                st[1] = [r for r in st[1] if r.is_dma or r.eng != eng]
            st[1].append(op)
        for k in writes:
            st = self._st(k)
            st[0] = op
            st[1] = []
        op.deps = dl
        if is_dma:
            if dma_key not in self.dma_sem_of:
                self.dma_sem_of[dma_key] = self.pool[len(self.dma_sem_of)]
                self.dma_count[dma_key] = 0
            self.dma_count[dma_key] += dinc
            op.dma_val = self.dma_count[dma_key]
        self.ops.append(op)
        return op

    @staticmethod
    def _need_wait(op, dep, kind):
        if dep.is_dma or op.is_dma or dep.eng != op.eng:
            return True
        if op.eng == "pe":
            return False
        return kind == "RAW"

    def flush(self):
        nc = self.nc
        ops = self.ops
        eng_ops = {e: [] for e in ENGS}
        for op in ops:
            eng_ops[op.eng].append(op)
        for op in ops:
            for dep, kind, _dv in op.deps:
                if not dep.is_dma and self._need_wait(op, dep, kind):
                    dep.needs_inc = True
        for e in ENGS:
            comp = [o for o in eng_ops[e] if not o.is_dma]
            if comp:
                comp[-1].needs_inc = True
            c = self.eng_count[e]
            for op in eng_ops[e]:
                if op.needs_inc and not op.is_dma:
                    c += 1
                    op.inc_val = c
            self.eng_count[e] = c
        self.bar_count += len(ENGS)
        bar_target = self.bar_count
        sems = self.sems

        def make(e):
            def body(eng):
                waited = {}
                my_dma = {}
                for op in eng_ops[e]:
                    for dep, kind, dv in op.deps:
                        if not self._need_wait(op, dep, kind):
                            continue
                        if dep.is_dma:
                            sem = self.dma_sem_of[dep.dma_key]
                            val = dv
                            key = ("d", dep.dma_key)
                        else:
                            sem = sems[dep.eng]
                            val = dep.inc_val
                            key = ("e", dep.eng)
                        if waited.get(key, 0) >= val:
                            continue
                        waited[key] = val
                        eng.wait_ge(sem, val)
                    ins = op.fn(eng)
                    if op.is_dma:
                        ins.then_inc(self.dma_sem_of[op.dma_key], op.dinc)
                        my_dma[op.dma_key] = op.dma_val
                    elif op.needs_inc:
                        ins.then_inc(sems[e], 1)
                for k, v in my_dma.items():
                    eng.wait_ge(self.dma_sem_of[k], v)
                if self.eng_count[e] > 0:
                    eng.wait_ge(sems[e], self.eng_count[e])
                eng.sem_inc(self.bar, 1)
                eng.wait_ge(self.bar, bar_target)
            return body

        with nc.Block() as block:
            block.tensor(make("pe"))
            block.scalar(make("act"))
            block.vector(make("dve"))
            block.gpsimd(make("pool"))
            block.sync(make("sp"))
        self.ops = []
        self.buf = {}


class Cfg:
    def __init__(self, LP, LSC, debug=False):
        self.LP = LP
        self.LSC = LSC
        self.NX = LP + LSC + 2 * HALO
        self.NY = LP + LSC
        self.debug = debug
        self.stop = 99
        self.NTP = LP // TT
        self.NTS = LSC // TT
        self.NTM = self.NTP + self.NTS
        self.main_rows = [i * TT for i in range(self.NTP)] + [LP + HALO + i * TT for i in range(self.NTS)]
        self.halo_rows = [LP, LP + HALO + LSC]
        self.VC = self.NX + 30

    def vcol(self, xrow):
        return xrow + (15 if xrow < self.LP else 30)

    def yrow(self, xrow):
        return xrow if xrow < self.LP else xrow - HALO


def build(cfg):
    nc = bass.Bass("TRN2", target_bir_lowering=False)
    NX, NY = cfg.NX, cfg.NY
    dbg = cfg.debug

    def din(name, shape):
        return nc.dram_tensor(name, list(shape), F32, kind="ExternalInput").ap()

    def dscr(name, shape, dt=F32):
        return nc.dram_tensor(name, list(shape), dt, kind=("ExternalOutput" if dbg else "Internal")).ap()

    x_d = din("x", [NX, D])
    mem_d = din("mem", [512, D])
    wgu1_d = din("wgu1", [D, 2 * DFF]); wd1_d = din("wd1", [DFF, D])
    wgu2_d = din("wgu2", [D, 2 * DFF]); wd2_d = din("wd2", [DFF, D])
    win_d = din("win", [D, 3584]); wpw_d = din("wpw", [512, D]); wglu_d = din("wglu", [512, 2 * D])
    wout_d = din("wout", [D, D]); wq_d = din("wq", [D, D]); wkv_d = din("wkv", [D, 2 * D]); wo_d = din("wo", [D, D])
    g_d = {n: din("g_" + n, [128, D]) for n in ("ffn1", "mix", "x", "mem", "ffn2", "fin")}
    binfm_d = din("binfm", [128, 28]); bssm_d = din("bssm", [1, 512])
    convw_d = din("convw", [128, 4 * 31]); convb_d = din("convb", [128, 4])
    lng_d = din("lng", [128, 4]); lnb_d = din("lnb", [128, 4])
    lamre_d = din("lamre", [128, 64]); lamim_d = din("lamim", [128, 64]); logdt_d = din("logdt", [128, 64])
    Bre_d = din("Bre", [128, 1024]); Bim_d = din("Bim", [128, 1024])
    Cre_d = din("Cre", [128, 1024]); Cim_d = din("Cim", [128, 1024])
    dsk_d = din("dsk", [128, 32])
    ident_d = din("ident", [128, 128])
    maskf_d = din("maskf", [128, 512]); maskb_d = din("maskb", [128, 512]); maskd_d = din("maskd", [128, 128])
    vmask_d = din("vmask", [128, 2 * HALO])
    ohf_d = din("ohf", [128, 7]); ohb_d = din("ohb", [128, 7])

    y_d = nc.dram_tensor("y", [NY, D], F32, kind="ExternalOutput").ap()

    h1s = dscr("h1s", [NX, D])
    vs = dscr("vs", [512, cfg.VC], BF16)
    gs = dscr("gs", [2048, NX], BF16)
    NTM = cfg.NTM
    Us = dscr("Us", [NTM, 128, G * KS], BF16)
    Sfs = dscr("Sfs", [NTM, 128, G * KS], BF16)
    Sbs = dscr("Sbs", [NTM, 128, G * KS], BF16)
    h2s = dscr("h2s", [NX, D])
    h3s = dscr("h3s", [NX, D])
    BBs = dscr("BBs", [4, 128, G * 128], BF16)
    CCs = dscr("CCs", [2, 128, G * 128], BF16)
    DDs = dscr("DDs", [128, G * 128], BF16)
    AAs = dscr("AAs", [4, 128, 128])
    cc_in = nc.dram_tensor("cc_in", [128, 128], F32, kind="Internal").ap()
    cc_out = nc.dram_tensor("cc_out", [NCORE * 128, 128], F32, kind="Internal").ap()

    top = ExitStack()
    S = Sched(nc, top)
    mult, add, sub = ALU.mult, ALU.add, ALU.subtract

    uniq = [0]

    def sb(st, name, shape, dt=F32):
        uniq[0] += 1
        return st.enter_context(nc.sbuf_tensor("sb%d_%s" % (uniq[0], name), list(shape), dt))

    def ps(st, name, shape, dt=F32):
        uniq[0] += 1
        return st.enter_context(nc.psum_tensor("ps%d_%s" % (uniq[0], name), list(shape), dt))

    def load(dst_ap, src_ap, key, writes, eng="sp", reads=()):
        S.add(eng, lambda e: e.dma_start(out=dst_ap, in_=src_ap), reads=reads, writes=writes, dma_key=key)

    def store(dst_ap, src_ap, key, reads, eng="sp", writes=()):
        S.add(eng, lambda e: e.dma_start(out=dst_ap, in_=src_ap), reads=reads, writes=writes, dma_key=key)

    def load_w_bf16(dst, src_d, nkt, key, ncols, col0=0, chunk=None):
        for kt in range(nkt):
            S.add("pool", lambda e, kt=kt: e.dma_start(out=dst[:, kt, :], in_=src_d[kt * 128:(kt + 1) * 128, col0:col0 + ncols],
                                                          max_dma_last_dim=8192),
                  writes=[(key, kt)], dma_key=key)

    def wkeys(key, nkt):
        return [(key, kt) for kt in range(nkt)]

    def norm_T(xt, xkey, gbc, ufm, ufm_key, tmp, ident, tpp, tag):
        ss, rs, un, junk = tmp["ss"], tmp["rs"], tmp["un"], tmp["junk"]
        for s in range(NS):
            S.add("act", lambda e, s=s: e.activation(out=junk[:, :], in_=xt[:, s, :], func=AF.Square, accum_out=ss[:, s:s + 1]),
                  reads=[(xkey, s)], writes=["junk", ("ss", s)])
        S.add("act", lambda e: e.activation(out=rs[:, 0:NS], in_=ss[:, 0:NS], func=AF.Sqrt, scale=1.0 / D, bias=tmp["epsc"][:, 0:1]),
              reads=[("ss", s) for s in range(NS)], writes=["rs"])
        S.add("dve", lambda e: e.reciprocal(out=rs[:, NS:2 * NS], in_=rs[:, 0:NS]), reads=["rs"], writes=["rs2"])
        for s in range(NS):
            S.add("dve", lambda e, s=s: e.scalar_tensor_tensor(out=un[:, s, :], in0=xt[:, s, :], scalar=rs[:, NS + s:NS + s + 1], in1=gbc[:, :],
                                                                  op0=mult, op1=mult),
                  reads=[(xkey, s), "rs2", tag + "_g"], writes=[("un", s)])
            tp = tpp[s % len(tpp)]
            tpk = ("tp", s % len(tpp))
            for kt in range(KT):
                S.add("pe", lambda e, s=s, kt=kt, tp=tp: e.transpose(out=tp[:, kt, :], in_=un[:, s, kt * 128:(kt + 1) * 128], identity=ident[:, :]),
                      reads=[("un", s), "ident"], writes=[tpk])
            S.add("act", lambda e, s=s, tp=tp: e.copy(out=ufm[:, :, s * 128:(s + 1) * 128], in_=tp[:, :, :]),
                  reads=[tpk], writes=[(ufm_key, s)])

    def norm_tmp(st):
        t = {"ss": sb(st, "n_ss", [128, NS]), "rs": sb(st, "n_rs", [128, 2 * NS]), "un": sb(st, "n_un", [128, NS, D], BF16),
             "junk": sb(st, "n_junk", [128, D]), "epsc": sb(st, "n_eps", [128, 1])}
        S.add("dve", lambda e: e.memset(t["epsc"][:, :], EPS), writes=["epsc"])
        return t

    def load_consts(st, gname):
        ident = sb(st, "ident", [128, 128], BF16)
        S.add("pool", lambda e: e.dma_start(out=ident[:, :], in_=ident_d), writes=["ident"], dma_key="ident")
        gbc = None
        if gname is not None:
            gbc = sb(st, "gbc_" + gname, [128, D])
            load(gbc[:, :], g_d[gname], "gbc", [gname + "_g"])
        return ident, gbc

    def ffn_pass(tiles, src_d, dst_rows_fn, dst_d, wgu_d, wd_d, gname, final):
        with ExitStack() as st:
            ident, gbc = load_consts(st, gname)
            gfin = None
            if final:
                gfin = sb(st, "gfin", [128, D])
                load(gfin[:, :], g_d["fin"], "gbc2", ["fin_g"])
            wgu = sb(st, "wgu", [128, KT, 2 * DFF], BF16)
            wd = sb(st, "wd", [128, NF, D], BF16)
            load_w_bf16(wgu, wgu_d, KT, "wgu", 2 * DFF)
            load_w_bf16(wd, wd_d, NF, "wd", D)
            tmp = norm_tmp(st)
            xts = [sb(st, "xt%d" % i, [128, NS, D]) for i in range(3)]
            ufm = sb(st, "ufm", [128, KT, TT], BF16)
            hid = sb(st, "hid", [128, NF, TT], BF16)
            sg = [sb(st, "sg%d" % i, [128, TT]) for i in range(2)]
            tpp = [ps(st, "tp%d" % i, [128, KT, 128], BF16) for i in range(2)]
            gup = [ps(st, "gu%d" % i, [128, 2, TT]) for i in range(2)]
            dn = [ps(st, "dn%d" % i, [128, 512]) for i in range(4)]
            yo = sb(st, "yo", [128, D]) if final else None

            def issue_load(i):
                xrow, ntok = tiles[i]
                xt = xts[i % 3]
                for s in range(ntok // 128):
                    load(xt[:, s, :], src_d[xrow + s * 128: xrow + (s + 1) * 128, :], "xld%d" % (i % 3), [(("xt", i % 3), s)])

            issue_load(0)
            for i, (xrow, ntok) in enumerate(tiles):
                if i + 1 < len(tiles):
                    issue_load(i + 1)
                xt = xts[i % 3]
                xkey = ("xt", i % 3)
                ns = ntok // 128
                assert ns == NS or ntok == 128
                nt = ntok
                if ns == NS:
                    norm_T(xt, xkey, gbc, ufm, "ufm", tmp, ident, tpp, gname)
                else:
                    norm_T_n(xt, xkey, gbc, ufm, "ufm", tmp, ident, tpp, gname, ns)
                ukeys = [("ufm", s) for s in range(ns)]
                for f in range(NF):
                    gu = gup[f % 2]
                    guk = ("gu", f % 2)
                    for half in range(2):
                        c0 = half * DFF + f * 128
                        for kt in range(KT):
                            S.add("pe", lambda e, gu=gu, half=half, c0=c0, kt=kt, nt=nt: e.matmul(gu[:, half, 0:nt], lhsT=wgu[:, kt, c0:c0 + 128], rhs=ufm[:, kt, 0:nt],
                                                                                                    start=(kt == 0), stop=(kt == KT - 1)),
                                  reads=ukeys + [("wgu", kt)], writes=[guk])
                    sgt = sg[f % 2]
                    S.add("act", lambda e, gu=gu, sgt=sgt, nt=nt: e.activation(out=sgt[:, 0:nt], in_=gu[:, 0, 0:nt], func=AF.Silu),
                          reads=[guk], writes=[("sg", f % 2)])
                    S.add("dve", lambda e, gu=gu, sgt=sgt, f=f, nt=nt: e.tensor_tensor(out=hid[:, f, 0:nt], in0=gu[:, 1, 0:nt], in1=sgt[:, 0:nt], op=mult),
                          reads=[guk, ("sg", f % 2)], writes=[("hid", f)])
                for s in range(ns):
                    for half in range(2):
                        dps = dn[(s * 2 + half) % 4]
                        dk = ("dn", (s * 2 + half) % 4)
                        for f in range(NF):
                            S.add("pe", lambda e, dps=dps, f=f, s=s, half=half: e.matmul(dps[:, :], lhsT=hid[:, f, s * 128:(s + 1) * 128], rhs=wd[:, f, half * 512:(half + 1) * 512],
                                                                                          start=(f == 0), stop=(f == NF - 1)),
                                  reads=[("hid", f), ("wd", f)], writes=[dk])
                        S.add("dve", lambda e, dps=dps, s=s, half=half, xt=xt: e.scalar_tensor_tensor(out=xt[:, s, half * 512:(half + 1) * 512], in0=dps[:, :], scalar=0.5,
                                                                                                        in1=xt[:, s, half * 512:(half + 1) * 512], op0=mult, op1=add),
                              reads=[dk, (xkey, s)], writes=[(xkey, s)])
                    drow = dst_rows_fn(xrow) + s * 128
                    if not final:
                        store(dst_d[drow:drow + 128, :], xt[:, s, :], "xst%d" % (i % 3), [(xkey, s)])
                    else:
                        ss, rs, junk = tmp["ss"], tmp["rs"], tmp["junk"]
                        S.add("act", lambda e, s=s, xt=xt: e.activation(out=junk[:, :], in_=xt[:, s, :], func=AF.Square, accum_out=ss[:, s:s + 1]),
                              reads=[(xkey, s)], writes=["junk", ("ss", s)])
                        S.add("act", lambda e, s=s: e.activation(out=rs[:, s:s + 1], in_=ss[:, s:s + 1], func=AF.Sqrt, scale=1.0 / D, bias=tmp["epsc"][:, 0:1]),
                              reads=[("ss", s)], writes=[("rsf", s)])
                        S.add("dve", lambda e, s=s: e.reciprocal(out=rs[:, NS + s:NS + s + 1], in_=rs[:, s:s + 1]), reads=[("rsf", s)], writes=[("rsf2", s)])
                        S.add("dve", lambda e, s=s, xt=xt: e.scalar_tensor_tensor(out=yo[:, :], in0=xt[:, s, :], scalar=rs[:, NS + s:NS + s + 1], in1=gfin[:, :],
                                                                                    op0=mult, op1=mult),
                              reads=[(xkey, s), ("rsf2", s), "fin_g"], writes=["yo"])
                        store(dst_d[drow:drow + 128, :], yo[:, :], "yst", ["yo"])
            S.flush()

    def norm_T_n(xt, xkey, gbc, ufm, ufm_key, tmp, ident, tpp, tag, ns):
        ss, rs, un, junk = tmp["ss"], tmp["rs"], tmp["un"], tmp["junk"]
        for s in range(ns):
            S.add("act", lambda e, s=s: e.activation(out=junk[:, :], in_=xt[:, s, :], func=AF.Square, accum_out=ss[:, s:s + 1]),
                  reads=[(xkey, s)], writes=["junk", ("ss", s)])
        S.add("act", lambda e: e.activation(out=rs[:, 0:ns], in_=ss[:, 0:ns], func=AF.Sqrt, scale=1.0 / D, bias=tmp["epsc"][:, 0:1]),
              reads=[("ss", s) for s in range(ns)], writes=["rs"])
        S.add("dve", lambda e: e.reciprocal(out=rs[:, NS:NS + ns], in_=rs[:, 0:ns]), reads=["rs"], writes=["rs2"])
        for s in range(ns):
            S.add("dve", lambda e, s=s: e.scalar_tensor_tensor(out=un[:, s, :], in0=xt[:, s, :], scalar=rs[:, NS + s:NS + s + 1], in1=gbc[:, :],
                                                                  op0=mult, op1=mult),
                  reads=[(xkey, s), "rs2", tag + "_g"], writes=[("un", s)])
            tp = tpp[s % len(tpp)]
            tpk = ("tp", s % len(tpp))
            for kt in range(KT):
                S.add("pe", lambda e, s=s, kt=kt, tp=tp: e.transpose(out=tp[:, kt, :], in_=un[:, s, kt * 128:(kt + 1) * 128], identity=ident[:, :]),
                      reads=[("un", s), "ident"], writes=[tpk])
            S.add("act", lambda e, s=s, tp=tp: e.copy(out=ufm[:, :, s * 128:(s + 1) * 128], in_=tp[:, :, :]),
                  reads=[tpk], writes=[(ufm_key, s)])

    def p2_pass():
        with ExitStack() as st:
            ident, gbc = load_consts(st, "mix")
            win = sb(st, "win", [128, KT, 3584], BF16)
            load_w_bf16(win, win_d, KT, "win", 3584)
            binfm = sb(st, "binfm", [128, 28]); load(binfm[:, :], binfm_d, "c1", ["binfm"])
            bssm = sb(st, "bssm", [1, 512], BF16)
            S.add("pool", lambda e: e.dma_start(out=bssm[:, :], in_=bssm_d), writes=["bssm"], dma_key="c2")
            ones1 = sb(st, "ones1", [1, 128], BF16)
            S.add("dve", lambda e: e.memset(ones1[:, :], 1.0), writes=["ones1"])
            vmask = sb(st, "vmask", [128, 2 * HALO]); load(vmask[:, :], vmask_d, "c3", ["vmask"])
            zeros = sb(st, "zeros", [128, 16], BF16)
            S.add("dve", lambda e: e.memset(zeros[:, :], 0.0), writes=["zeros"])
            for ct in range(4):
                store(vs[ct * 128:(ct + 1) * 128, 0:15], zeros[:, 0:15], "vz", ["zeros"])
                store(vs[ct * 128:(ct + 1) * 128, 15 + cfg.LP:30 + cfg.LP], zeros[:, 0:15], "vz", ["zeros"])
            tmp = norm_tmp(st)
            xts = [sb(st, "xt%d" % i, [128, NS, D]) for i in range(2)]
            ufm = sb(st, "ufm", [128, KT, TT], BF16)
            sg = [sb(st, "sg%d" % i, [128, TT]) for i in range(2)]
            vt = [sb(st, "vt%d" % i, [128, 4, TT], BF16) for i in range(2)]
            gt = [sb(st, "gt%d" % i, [128, 16, TT], BF16) for i in range(2)]
            Zsb = sb(st, "Zsb", [KS, G, 128], BF16)
            Usb = [sb(st, "Usb%d" % i, [128, G, KS], BF16) for i in range(2)]
            tpp = [ps(st, "tp%d" % i, [128, KT, 128], BF16) for i in range(1)]
            zf = [ps(st, "zf%d" % i, [128, 2, TT]) for i in range(3)]
            zps = [ps(st, "zps%d" % i, [KS, 512]) for i in range(2)]
            tpu = [ps(st, "tpu%d" % i, [128, 16, KS], BF16) for i in range(2)]
            tiles = [(r, TT, mi) for mi, r in enumerate(cfg.main_rows)] + [(r, HALO, -1 - hi) for hi, r in enumerate(cfg.halo_rows)]

            def issue_load(i):
                xrow, ntok, _ = tiles[i]
                for s in range(ntok // 128):
                    load(xts[i % 2][:, s, :], h1s[xrow + s * 128: xrow + (s + 1) * 128, :], "xld%d" % (i % 2), [(("xt", i % 2), s)])

            issue_load(0)
            zi = 0
            for i, (xrow, nt, mi) in enumerate(tiles):
                if i + 1 < len(tiles):
                    issue_load(i + 1)
                xt = xts[i % 2]; xkey = ("xt", i % 2)
                ns = nt // 128
                norm_T_n(xt, xkey, gbc, ufm, "ufm", tmp, ident, tpp, "mix", ns)
                ukeys = [("ufm", s) for s in range(ns)]
                vtt = vt[i % 2]; vk = ("vt", i % 2)
                for ct in range(4):
                    z = zf[zi % 3]; zk = ("zf", zi % 3); zi += 1
                    for half, ot in ((0, ct), (1, 4 + ct)):
                        for kt in range(KT):
                            S.add("pe", lambda e, z=z, half=half, ot=ot, kt=kt, nt=nt: e.matmul(z[:, half, 0:nt], lhsT=win[:, kt, ot * 128:(ot + 1) * 128], rhs=ufm[:, kt, 0:nt],
                                                                                                  start=(kt == 0), stop=(kt == KT - 1)),
                                  reads=ukeys + [("win", kt)], writes=[zk])
                    sgt = sg[ct % 2]
                    S.add("act", lambda e, z=z, sgt=sgt, ct=ct, nt=nt: e.activation(out=sgt[:, 0:nt], in_=z[:, 1, 0:nt], func=AF.Sigmoid, bias=binfm[:, 4 + ct:5 + ct]),
                          reads=[zk, "binfm"], writes=[("sg", ct % 2)])
                    S.add("dve", lambda e, z=z, sgt=sgt, ct=ct, nt=nt, vtt=vtt: e.scalar_tensor_tensor(out=vtt[:, ct, 0:nt], in0=z[:, 0, 0:nt], scalar=binfm[:, ct:ct + 1], in1=sgt[:, 0:nt],
                                                                                                      op0=add, op1=mult),
                          reads=[zk, ("sg", ct % 2), "binfm"], writes=[(vk, ct)])
                    if mi < 0:
                        hi = -1 - mi
                        S.add("dve", lambda e, ct=ct, vtt=vtt, hi=hi: e.tensor_tensor(out=vtt[:, ct, 0:HALO], in0=vtt[:, ct, 0:HALO], in1=vmask[:, hi * HALO:(hi + 1) * HALO], op=mult),
                              reads=[(vk, ct), "vmask"], writes=[(vk, ct)])
                    c0 = cfg.vcol(xrow)
                    store(vs[ct * 128:(ct + 1) * 128, c0:c0 + nt], vtt[:, ct, 0:nt], "st%d" % (i % 2), [(vk, ct)])
                if mi < 0:
                    continue
                gtt = gt[i % 2]; gk = ("gt", i % 2)
                for o2 in range(8):
                    z = zf[zi % 3]; zk = ("zf", zi % 3); zi += 1
                    for half in range(2):
                        ot = 12 + o2 * 2 + half
                        for kt in range(KT):
                            S.add("pe", lambda e, z=z, half=half, ot=ot, kt=kt: e.matmul(z[:, half, :], lhsT=win[:, kt, ot * 128:(ot + 1) * 128], rhs=ufm[:, kt, :],
                                                                                          start=(kt == 0), stop=(kt == KT - 1)),
                                  reads=ukeys + [("win", kt)], writes=[zk])
                    for half in range(2):
                        ot = 12 + o2 * 2 + half
                        S.add("act", lambda e, z=z, half=half, ot=ot, gtt=gtt: e.activation(out=gtt[:, ot - 12, :], in_=z[:, half, :], func=AF.Sigmoid, bias=binfm[:, ot:ot + 1]),
                              reads=[zk, "binfm"], writes=[(gk, ot - 12)])
                store(gs[:, xrow:xrow + TT].rearrange("(o p) t -> p o t", p=128), gtt[:, :, :], "st%d" % (i % 2), [(gk, o) for o in range(16)])
                for j in range(8):
                    zp = zps[j % 2]; zpk = ("zps", j % 2)
                    for kt in range(KT):
                        S.add("pe", lambda e, zp=zp, j=j, kt=kt: e.matmul(zp[:, :], lhsT=ufm[:, kt, j:TT:8], rhs=win[:, kt, 1024:1536], start=(kt == 0), stop=False),
                              reads=ukeys + [("win", kt)], writes=[zpk])
                    S.add("pe", lambda e, zp=zp: e.matmul(zp[:, :], lhsT=ones1[0:1, 0:KS], rhs=bssm[0:1, :], start=False, stop=True),
                          reads=["ones1", "bssm"], writes=[zpk])
                    S.add("act", lambda e, zp=zp, j=j: e.copy(out=Zsb[:, :, j * 16:(j + 1) * 16], in_=zp[:, :].rearrange("k (g h) -> k g h", h=16)), reads=[zpk], writes=[("Zsb", j)])
                usb = Usb[i % 2]; uk = ("Usb", i % 2)
                for gh in range(2):
                    tp = tpu[gh]; tk = ("tpu", gh)
                    for gl in range(16):
                        g = gh * 16 + gl
                        S.add("pe", lambda e, tp=tp, gl=gl, g=g: e.transpose(out=tp[:, gl, :], in_=Zsb[:, g, :], identity=ident[0:KS, 0:KS]),
                              reads=[("Zsb", j) for j in range(8)] + ["ident"], writes=[tk])
                    S.add("dve", lambda e, tp=tp, gh=gh, usb=usb: e.tensor_copy(out=usb[:, gh * 16:(gh + 1) * 16, :], in_=tp[:, :, :]),
                          reads=[tk], writes=[(uk, gh)])
                store(Us[mi].rearrange("p (g k) -> p g k", g=G), usb[:, :, :], "st%d" % (i % 2), [(uk, 0), (uk, 1)])
            S.flush()

    def prep_pass():
        nsq = int(round(math.log2(cfg.LSC // 8)))
        assert 8 * (2 ** nsq) == cfg.LSC
        with ExitStack() as st:
            ident, _ = load_consts(st, None)
            cnt = [0]

            def T(n=64, dt=F32):
                cnt[0] += 1
                return sb(st, "pt%d" % cnt[0], [128, n], dt)

            def ld(dram, n):
                t = T(n)
                load(t[:, :], dram, "pl", [id(t)])
                return t

            lamre = ld(lamre_d, 64); lamim = ld(lamim_d, 64); logdt = ld(logdt_d, 64)
            Bre = ld(Bre_d, 1024); Bim = ld(Bim_d, 1024); Cre = ld(Cre_d, 1024); Cim = ld(Cim_d, 1024)
            dsk = ld(dsk_d, 32); maskf = ld(maskf_d, 512); maskb = ld(maskb_d, 512); maskd = ld(maskd_d, 128)

            def K_(a):
                return id(a.tensor) if hasattr(a, "tensor") else id(a)

            def tt(o, a, b, op, eng="dve"):
                S.add(eng, lambda e: e.tensor_tensor(out=o, in0=a, in1=b, op=op), reads=[K_(a), K_(b)], writes=[K_(o)])

            def ts(o, a, s1, s2, op0, op1):
                S.add("dve", lambda e: e.tensor_scalar(out=o, in0=a, scalar1=s1, scalar2=s2, op0=op0, op1=op1), reads=[K_(a)], writes=[K_(o)])

            def stt(o, a, sc, b, op0, op1):
                rd = [K_(a), K_(b)] + ([K_(sc)] if not isinstance(sc, float) else [])
                S.add("dve", lambda e: e.scalar_tensor_tensor(out=o, in0=a, scalar=sc, in1=b, op0=op0, op1=op1), reads=rd, writes=[K_(o)])

            def act(o, a, func, scale=1.0):
                S.add("act", lambda e: e.activation(out=o, in_=a, func=func, scale=scale), reads=[K_(a)], writes=[K_(o)])

            def cp(o, a, eng):
                if eng == "act":
                    S.add("act", lambda e: e.copy(out=o, in_=a), reads=[K_(a)], writes=[K_(o)])
                else:
                    S.add(eng, lambda e: e.tensor_copy(out=o, in_=a), reads=[K_(a)], writes=[K_(o)])

            A = lambda t: t[:, :]
            dt_ = T(); act(A(dt_), A(logdt), AF.Exp)
            xr = T(); tt(A(xr), A(lamre), A(dt_), mult)
            er = T(); act(A(er), A(xr), AF.Exp)
            th = T(); tt(A(th), A(lamim), A(dt_), mult)

            def sin_red(shift):
                t = T(); ts(A(t), A(th), 16.0 * TWO_PI + shift, None, add, ALU.bypass)
                q = T(); ts(A(q), A(t), 1.0 / TWO_PI, 0.5, mult, add)
                qi = T(64, I32); cp(A(qi), A(q), "dve")
                qf = T(); cp(A(qf), A(qi), "dve")
                r = T(); stt(A(r), A(qf), -TWO_PI, A(t), mult, add)
                m = T()
                S.add("dve", lambda e: e.tensor_single_scalar(out=A(m), in_=A(r), scalar=math.pi, op=ALU.is_gt), reads=[K_(r)], writes=[K_(m)])
                r2 = T(); stt(A(r2), A(m), -TWO_PI, A(r), mult, add)
                m2 = T()
                S.add("dve", lambda e: e.tensor_single_scalar(out=A(m2), in_=A(r2), scalar=-math.pi, op=ALU.is_lt), reads=[K_(r2)], writes=[K_(m2)])
                r3 = T(); stt(A(r3), A(m2), TWO_PI, A(r2), mult, add)
                r4 = T(); ts(A(r4), A(r3), -3.1415925, 3.1415925, ALU.max, ALU.min)
                o = T(); act(A(o), A(r4), AF.Sin)
                return o

            sinv = sin_red(0.0)
            cosv = sin_red(math.pi / 2)
            PWR = sb(st, "PWR", [128, 9, 64]); PWI = sb(st, "PWI", [128, 9, 64])
            IPR = sb(st, "IPR", [128, 8, 64]); IPI = sb(st, "IPI", [128, 8, 64])
            tmps = [T(1024) for _ in range(4)]

            def cmul(ore, oim, are, aim, bre, bim, n):
                t1, t2, t3, t4 = [t[:, 0:n] if len(ore.shape) == 2 else t[:, 0:n].rearrange("p (a b) -> p a b", b=ore.shape[2]) for t in tmps]
                tt(t1, are, bre, mult); tt(t2, aim, bim, mult); tt(ore, t1, t2, sub)
                tt(t3, are, bim, mult); tt(t4, aim, bre, mult); tt(oim, t3, t4, add)

            S.add("dve", lambda e: e.memset(PWR[:, 0, :], 1.0), writes=[K_(PWR)])
            S.add("dve", lambda e: e.memset(PWI[:, 0, :], 0.0), writes=[K_(PWI)])
            S.add("dve", lambda e: e.memset(IPR[:, 0, :], 1.0), writes=[K_(IPR)])
            S.add("dve", lambda e: e.memset(IPI[:, 0, :], 0.0), writes=[K_(IPI)])
            tt(PWR[:, 1, :], A(er), A(cosv), mult); tt(PWI[:, 1, :], A(er), A(sinv), mult)
            for m in range(1, 8):
                cmul(PWR[:, m + 1, :], PWI[:, m + 1, :], PWR[:, m, :], PWI[:, m, :], PWR[:, 1, :], PWI[:, 1, :], 64)
            e2 = T(); tt(A(e2), A(er), A(er), mult)
            re2 = T(); S.add("dve", lambda e: e.reciprocal(out=A(re2), in_=A(e2)), reads=[K_(e2)], writes=[K_(re2)])
            tt(IPR[:, 1, :], PWR[:, 1, :], A(re2), mult)
            nim = T(); ts(A(nim), PWI[:, 1, :], -1.0, None, mult, ALU.bypass)
            tt(IPI[:, 1, :], A(nim), A(re2), mult)
            for m in range(1, 7):
                cmul(IPR[:, m + 1, :], IPI[:, m + 1, :], IPR[:, m, :], IPI[:, m, :], IPR[:, 1, :], IPI[:, 1, :], 64)
            nr = T(); ts(A(nr), PWR[:, 1, :], -1.0, None, add, ALU.bypass)
            d1 = T(); tt(A(d1), A(lamre), A(lamre), mult)
            d2 = T(); tt(A(d2), A(lamim), A(lamim), mult)
            den = T(); tt(A(den), A(d1), A(d2), add)
            rden = T(); S.add("dve", lambda e: e.reciprocal(out=A(rden), in_=A(den)), reads=[K_(den)], writes=[K_(rden)])
            nlamim = T(); ts(A(nlamim), A(lamim), -1.0, None, mult, ALU.bypass)
            qre0 = T(); qim0 = T()
            cmul(A(qre0), A(qim0), A(nr), PWI[:, 1, :], A(lamre), A(nlamim), 64)
            qre = T(); qim = T()
            tt(A(qre), A(qre0), A(rden), mult); tt(A(qim), A(qim0), A(rden), mult)
            BBr = T(1024); BBi = T(1024)
            v3 = lambda t: t[:, :].rearrange("p (a b) -> p a b", b=16)
            qb = lambda t: t[:, :].to_broadcast([128, 64, 16])
            cmul(v3(BBr), v3(BBi), qb(qre), qb(qim), v3(Bre), v3(Bim), 1024)
            Tb = [[sb(st, "Tb%d%d" % (d, f), [128, G, 128], BF16) for f in range(2)] for d in range(2)]
            Tc = [sb(st, "Tc%d" % d, [128, G, 128], BF16) for d in range(2)]
            TX = [sb(st, "TX%d" % d, [128, G, 128], BF16) for d in range(2)]
            TY = [sb(st, "TY%d" % d, [128, G, 128], BF16) for d in range(2)]
            PRe = [T(512) for _ in range(2)]; PIm = [T(512) for _ in range(2)]
            pc = [0]

            def product(pwr, pwi, d, m, vr, vi, places):
                i = pc[0] % 2; pc[0] += 1
                pr = pwr[:, m, d * 32:(d + 1) * 32].to_broadcast([128, 32, 16])
                pi = pwi[:, m, d * 32:(d + 1) * 32].to_broadcast([128, 32, 16])
                br = vr[:, d * 512:(d + 1) * 512].rearrange("p (a b) -> p a b", b=16)
                bi = vi[:, d * 512:(d + 1) * 512].rearrange("p (a b) -> p a b", b=16)
                ore = PRe[i][:, :].rearrange("p (a b) -> p a b", b=16); oim = PIm[i][:, :].rearrange("p (a b) -> p a b", b=16)
                cmul(ore, oim, pr, pi, br, bi, 512)
                k = 0
                for (tab, j, tsrc, bsrc) in places:
                    for (lo, hi, src) in ((0, 64, tsrc), (64, 128, bsrc)):
                        srcap = (PRe[i] if src == "re" else PIm[i])[lo:hi, :].rearrange("p (a b) -> p a b", b=16)
                        dst = tab[lo:hi, :, j * 16:(j + 1) * 16]
                        if src == "-im":
                            S.add("act", lambda e, dst=dst, srcap=srcap: e.activation(out=dst, in_=srcap, func=AF.Copy, scale=-1.0),
                                  reads=[K_(srcap)], writes=[K_(tab)])
                        else:
                            cp(dst, srcap, "pool" if k % 2 == 0 else "act")
                        k += 1

            for m in range(8):
                product(PWR, PWI, 0, m, BBr, BBi, [(Tb[0][0], 7 - m, "re", "im"), (Tb[0][1], 7 - m, "im", "re")])
                product(IPR, IPI, 0, m, BBr, BBi, [(TX[0], m, "re", "im")])
                product(PWR, PWI, 1, m, BBr, BBi, [(Tb[1][0], m, "re", "im"), (Tb[1][1], m, "im", "re"), (TX[1], m, "re", "im")])
                product(IPR, IPI, 1, m, Cre, Cim, [(TY[1], m, "re", "-im")])
            for m in range(9):
                pl = []
                if m >= 1:
                    pl.append((Tc[0], m - 1, "re", "-im"))
                if m <= 7:
                    pl.append((TY[0], m, "re", "-im"))
                product(PWR, PWI, 0, m, Cre, Cim, pl)
                if m >= 1:
                    product(PWR, PWI, 1, m, Cre, Cim, [(Tc[1], 8 - m, "re", "-im")])
            for d in range(2):
                store(CCs[d], Tc[d][:, :, :].rearrange("p g c -> p (g c)"), "pst", [K_(Tc[d])])
            tpb = [ps(st, "tpb%d" % i, [128, 8, 128], BF16) for i in range(2)]
            stg = [sb(st, "stg%d" % i, [128, G, 128], BF16) for i in range(2)]
            bi_ = 0
            for d in range(2):
                for f in range(2):
                    sg_ = stg[(d * 2 + f) % 2]
                    for gb in range(4):
                        tp = tpb[bi_ % 2]; tk = ("tpb", bi_ % 2); bi_ += 1
                        for gl in range(8):
                            g = gb * 8 + gl
                            S.add("pe", lambda e, tp=tp, gl=gl, g=g, d=d, f=f: e.transpose(out=tp[:, gl, :], in_=Tb[d][f][:, g, :], identity=ident[:, :]),
                                  reads=[K_(Tb[d][f]), "ident"], writes=[tk])
                        S.add("dve", lambda e, tp=tp, gb=gb, sg_=sg_: e.tensor_copy(out=sg_[:, gb * 8:(gb + 1) * 8, :], in_=tp[:, :, :]),
                              reads=[tk], writes=[K_(sg_)])
                    store(BBs[d * 2 + f], sg_[:, :, :].rearrange("p g c -> p (g c)"), "pst2", [K_(sg_)])
            dps = [ps(st, "dps%d" % i, [128, 4, 128]) for i in range(4)]
            Dsb = sb(st, "Dsb", [128, G, 128], BF16)
            t1 = T(512); t2 = T(512)
            for gb in range(8):
                pf = dps[(gb % 2) * 2]; pb = dps[(gb % 2) * 2 + 1]
                kf = ("dps", (gb % 2) * 2); kb = ("dps", (gb % 2) * 2 + 1)
                for gl in range(4):
                    g = gb * 4 + gl
                    S.add("pe", lambda e, pf=pf, gl=gl, g=g: e.matmul(pf[:, gl, :], lhsT=TX[0][:, g, :], rhs=TY[0][:, g, :], start=True, stop=True),
                          reads=[K_(TX[0]), K_(TY[0])], writes=[kf])
                    S.add("pe", lambda e, pb=pb, gl=gl, g=g: e.matmul(pb[:, gl, :], lhsT=TX[1][:, g, :], rhs=TY[1][:, g, :], start=True, stop=True),
                          reads=[K_(TX[1]), K_(TY[1])], writes=[kb])
                S.add("dve", lambda e, pf=pf: e.tensor_tensor(out=t1[:, :], in0=pf[:, :, :].rearrange("p a b -> p (a b)"), in1=maskf[:, :], op=mult),
                      reads=[kf, K_(maskf)], writes=[K_(t1)])
                S.add("dve", lambda e, pb=pb: e.tensor_tensor(out=t2[:, :], in0=pb[:, :, :].rearrange("p a b -> p (a b)"), in1=maskb[:, :], op=mult),
                      reads=[kb, K_(maskb)], writes=[K_(t2)])
                tt(t1[:, :], t1[:, :], t2[:, :], add)
                for gl in range(4):
                    g = gb * 4 + gl
                    stt(Dsb[:, g, :], maskd[:, :], dsk[:, g:g + 1], t1[:, gl * 128:(gl + 1) * 128], mult, add)
            store(DDs, Dsb[:, :, :].rearrange("p g c -> p (g c)"), "pst3", [K_(Dsb)])
            sgn = T(1)
            S.add("dve", lambda e: e.memset(sgn[0:64, :], -1.0), writes=[K_(sgn)])
            S.add("dve", lambda e: e.memset(sgn[64:128, :], 1.0), writes=[K_(sgn)])
            AA = sb(st, "AA", [128, 4, 128])

            def build_A(i1, i2, pr, pi):
                aw = T()
                ts(A(aw), pi, sgn[:, 0:1], None, mult, ALU.bypass)
                naw = T(); ts(A(naw), A(aw), -1.0, None, mult, ALU.bypass)
                for d in range(2):
                    for f in range(2):
                        o = (d * 2 + f) * 32
                        cp(AA[:, i1, o:o + 32], pr[:, d * 32:(d + 1) * 32], "dve")
                        cp(AA[:, i2, o:o + 32], (aw if f == 0 else naw)[:, d * 32:(d + 1) * 32], "dve")

            build_A(0, 1, PWR[:, 8, :], PWI[:, 8, :])
            cr = T(); ci = T()
            cp(A(cr), PWR[:, 8, :], "dve"); cp(A(ci), PWI[:, 8, :], "dve")
            for _ in range(nsq):
                nr2 = T(); ni2 = T()
                cmul(A(nr2), A(ni2), A(cr), A(ci), A(cr), A(ci), 64)
                cr, ci = nr2, ni2
            build_A(2, 3, A(cr), A(ci))
            store(AAs.rearrange("a p c -> p a c"), AA[:, :, :], "pst4", [K_(AA)])
            S.flush()

    def scan_pass():
        NTP, NTS = cfg.NTP, cfg.NTS
        with ExitStack() as st:
            BB = [sb(st, "BB%d" % i, [128, G, 128], BF16) for i in range(4)]
            for i in range(4):
                load(BB[i][:, :, :].rearrange("p g c -> p (g c)"), BBs[i], "bbl", [("BB", i)])
            AA = sb(st, "AA", [128, 4, 128])
            load(AA[:, :, :], AAs.rearrange("a p c -> p a c"), "aal", ["AA"])
            ohf = sb(st, "ohf", [128, 7]); load(ohf[:, :], ohf_d, "ohl", ["ohf"])
            ohb = sb(st, "ohb", [128, 7]); load(ohb[:, :], ohb_d, "ohl", ["ohb"])
            Uf = [sb(st, "Uf%d" % i, [128, G, KS], BF16) for i in range(2)]
            Ub = [sb(st, "Ub%d" % i, [128, G, KS], BF16) for i in range(2)]
            Vall = [sb(st, "Vall%d" % i, [128, KS, 128]) for i in range(2)]
            XH = [sb(st, "XH%d" % i, [128, KS + 1, 128]) for i in range(2)]
            t1 = sb(st, "t1", [128, 128]); t2 = sb(st, "t2", [128, 128]); t3 = sb(st, "t3", [128, 128])
            So = [sb(st, "So%d" % i, [128, 2, G, KS], BF16) for i in range(2)]
            vps = [ps(st, "vps%d" % i, [128, 16, KS]) for i in range(4)]
            X0 = sb(st, "X0", [128, 128])
            S.add("dve", lambda e: e.memset(X0[:, :], 0.0), writes=["X0"])

            def step(xin, xin_keys, xout, xout_keys, a1, a2, vadds, vkeys):
                S.add("dve", lambda e: e.tensor_tensor(out=t1[:, :], in0=a1, in1=xin, op=mult), reads=xin_keys + ["AA"], writes=["t1"])
                xi = xin.rearrange("p (d f g) -> p d f g", d=2, f=2)
                a2v = a2.rearrange("p (d f g) -> p d f g", d=2, f=2)
                t2v = t2[:, :].rearrange("p (d f g) -> p d f g", d=2, f=2)
                S.add("dve", lambda e: e.tensor_tensor(out=t2v[:, :, 0, :], in0=a2v[:, :, 0, :], in1=xi[:, :, 1, :], op=mult), reads=xin_keys + ["AA"], writes=["t2a"])
                S.add("dve", lambda e: e.tensor_tensor(out=t2v[:, :, 1, :], in0=a2v[:, :, 1, :], in1=xi[:, :, 0, :], op=mult), reads=xin_keys + ["AA"], writes=["t2b"])
                S.add("dve", lambda e: e.tensor_tensor(out=t3[:, :], in0=t1[:, :], in1=t2[:, :], op=add), reads=["t1", "t2a", "t2b"], writes=["t3"])
                for n, (lo, hi, vap) in enumerate(vadds):
                    S.add("dve", lambda e, lo=lo, hi=hi, vap=vap: e.tensor_tensor(out=xout[:, lo:hi], in0=t3[:, lo:hi], in1=vap, op=add),
                          reads=["t3"] + vkeys, writes=[xout_keys[n]])

            def run_seq(tiles, x0_ap, x0_key, outputs, tag):
                n = len(tiles)
                prev, prev_key = x0_ap, [x0_key]

                def issue(t):
                    load(Uf[t % 2][:, :, :].rearrange("p g k -> p (g k)"), Us[tiles[t]], "ufl%d" % (t % 2), [("Uf", t % 2)])
                    load(Ub[t % 2][:, :, :].rearrange("p g k -> p (g k)"), Us[tiles[n - 1 - t]], "ubl%d" % (t % 2), [("Ub", t % 2)])

                issue(0)
                vi = 0
                for t in range(n):
                    if t + 1 < n:
                        issue(t + 1)
                    va = Vall[t % 2]; vk = ("Vall", t % 2)
                    for d in range(2):
                        U = (Uf if d == 0 else Ub)[t % 2]
                        ukey = ("Uf" if d == 0 else "Ub", t % 2)
                        for f in range(2):
                            for gh in range(2):
                                vp = vps[vi % 4]; vpk = ("vps", vi % 4); vi += 1
                                for gl in range(16):
                                    g = gh * 16 + gl
                                    S.add("pe", lambda e, vp=vp, gl=gl, g=g, d=d, f=f, U=U: e.matmul(vp[:, gl, :], lhsT=BB[d * 2 + f][:, g, :], rhs=U[:, g, :], start=True, stop=True),
                                          reads=[ukey, ("BB", d * 2 + f)], writes=[vpk])
                                c0 = (d * 2 + f) * 32 + gh * 16
                                src = vp[:, :, :] if d == 0 else vp[:, :, ::-1]
                                S.add("act", lambda e, src=src, va=va, c0=c0: e.copy(out=va[:, :, c0:c0 + 16], in_=src.rearrange("p g k -> p k g")),
                                      reads=[vpk], writes=[(vk, c0)])
                    vkeys = [(vk, c0) for c0 in range(0, 128, 16)]
                    xh = XH[t % 2]; xk = ("XH", t % 2)
                    for i in range(KS):
                        if i == 0:
                            xin, xin_keys = prev, prev_key
                        else:
                            xin, xin_keys = xh[:, i, :], [("XH", t % 2, i)]
                        step(xin, xin_keys, xh[:, i + 1, :], [("XH", t % 2, i + 1)], AA[:, 0, :], AA[:, 1, :], [(0, 128, va[:, i, :])], vkeys)
                        if i == 0:
                            S.add("act", lambda e, xin=xin, xh=xh: e.copy(out=xh[:, 0, :], in_=xin), reads=xin_keys, writes=[("XH", t % 2, 0)])
                    prev, prev_key = xh[:, KS, :], [("XH", t % 2, KS)]
                    if outputs:
                        so = So[t % 2]; sk = ("So", t % 2)
                        hk = [("XH", t % 2, i) for i in range(KS + 1)]
                        S.add("act", lambda e, so=so, xh=xh: e.copy(out=so[:, 0, :, :], in_=xh[:, 0:KS, 0:32].rearrange("p k g -> p g k")), reads=hk, writes=[(sk, 0)])
                        S.add("act", lambda e, so=so, xh=xh: e.copy(out=so[:, 1, :, :], in_=xh[:, KS - 1::-1, 64:96].rearrange("p k g -> p g k")), reads=hk, writes=[(sk, 1)])
                        store(Sfs[tiles[t]].rearrange("p (g k) -> p g k", g=G), so[:, 0, :, :], "sst%d" % (t % 2), [(sk, 0)])
                        store(Sbs[tiles[n - 1 - t]].rearrange("p (g k) -> p g k", g=G), so[:, 1, :, :], "sst%d" % (t % 2), [(sk, 1)])
                return prev, prev_key

            stiles = list(range(NTP, NTP + NTS))
            ptiles = list(range(NTP))
            xe, xek = run_seq(stiles, X0[:, :], "X0", False, "s1")
            store(cc_in, xe, "ccst", xek, writes=["cc_in"])
            S.add("pool", lambda e: e.collective_compute("AllGather", ALU.bypass, replica_groups=[list(range(NCORE))], ins=[cc_in], outs=[cc_out]),
                  reads=["cc_in"], writes=["cc_out"], dma_key="ccag", dinc=1)
            run_seq(ptiles, X0[:, :], "X0", True, "p")
            E = sb(st, "E", [128, NCORE, 128])
            load(E[:, :, :], cc_out.rearrange("(c p) f -> p c f", p=128), "ccld", ["E"], reads=["cc_out"])
            HC = sb(st, "HC", [128, NCORE, 128])
            S.add("dve", lambda e: e.memset(HC[:, 0, :], 0.0), writes=[("HC", 0, 0), ("HC", 0, 1)])
            for i in range(7):
                step(HC[:, i, :], [("HC", i, 0), ("HC", i, 1)], HC[:, i + 1, :], [("HC", i + 1, 0), ("HC", i + 1, 1)], AA[:, 2, :], AA[:, 3, :],
                     [(0, 64, E[:, i, 0:64]), (64, 128, E[:, 7 - i, 64:128])], ["E"])
            XS = sb(st, "XS", [128, 128])
            S.add("dve", lambda e: e.memset(XS[:, :], 0.0), writes=["XS"])
            for i in range(7):
                S.add("dve", lambda e, i=i: e.scalar_tensor_tensor(out=XS[:, 0:64], in0=HC[:, i + 1, 0:64], scalar=ohf[:, i:i + 1], in1=XS[:, 0:64], op0=mult, op1=add),
                      reads=[("HC", i + 1, 0), ("HC", i + 1, 1), "ohf", "XS"], writes=["XS"])
                S.add("dve", lambda e, i=i: e.scalar_tensor_tensor(out=XS[:, 64:128], in0=HC[:, i + 1, 64:128], scalar=ohb[:, i:i + 1], in1=XS[:, 64:128], op0=mult, op1=add),
                      reads=[("HC", i + 1, 0), ("HC", i + 1, 1), "ohb", "XS"], writes=["XS"])
            run_seq(stiles, XS[:, :], "XS", True, "s2")
            S.flush()

    def p3a_pass():
        with ExitStack() as st:
            ident, _ = load_consts(st, None)
            CC = [sb(st, "CC%d" % d, [128, G, 128], BF16) for d in range(2)]
            DDt = sb(st, "DDt", [128, G, 128], BF16)
            for d in range(2):
                load(CC[d][:, :, :].rearrange("p g c -> p (g c)"), CCs[d], "ccl", [("CC", d)])
            load(DDt[:, :, :].rearrange("p g c -> p (g c)"), DDs, "ccl", ["DD"])
            wglu = sb(st, "wglu", [128, 4, 2 * D], BF16); load_w_bf16(wglu, wglu_d, 4, "wglu", 2 * D)
            wpw = sb(st, "wpw", [128, 4, D], BF16); load_w_bf16(wpw, wpw_d, 4, "wpw", D)
            wout = sb(st, "wout", [128, KT, D], BF16); load_w_bf16(wout, wout_d, KT, "wout", D)
            convw = sb(st, "convw", [128, 124]); load(convw[:, :], convw_d, "c1", ["convw"])
            convb = sb(st, "convb", [128, 4]); load(convb[:, :], convb_d, "c1", ["convb"])
            lng = sb(st, "lng", [128, 4]); load(lng[:, :], lng_d, "c1", ["lng"])
            lnb = sb(st, "lnb", [128, 4]); load(lnb[:, :], lnb_d, "c1", ["lnb"])
            onesD = sb(st, "onesD", [128, 128], BF16)
            S.add("dve", lambda e: e.memset(onesD[:, :], 1.0 / 512.0), writes=["onesD"])
            epsc = sb(st, "epsc", [128, 1])
            S.add("dve", lambda e: e.memset(epsc[:, :], EPS), writes=["epsc"])
            xts = [sb(st, "xt%d" % i, [128, NS, D]) for i in range(2)]
            Ut = [sb(st, "Ut%d" % i, [128, 3, G, KS], BF16) for i in range(2)]
            gtt = [sb(st, "gtt%d" % i, [128, 16, TT], BF16) for i in range(2)]
            vw = [sb(st, "vw%d" % i, [128, 4, TT + 30], BF16) for i in range(2)]
            Ytok = sb(st, "Ytok", [KS, 8, 512], BF16)
            yfm = sb(st, "yfm", [128, 4, TT]); y2 = sb(st, "y2", [128, 4, TT]); ysg = sb(st, "ysg", [128, 4, TT])
            ygb = sb(st, "ygb", [128, 4, TT], BF16)
            cc_ = sb(st, "cc_", [128, 4, TT]); cbf = sb(st, "cbf", [128, 4, TT], BF16); csq = sb(st, "csq", [128, 4, TT], BF16)
            cact = sb(st, "cact", [128, 4, TT], BF16); ctm = sb(st, "ctm", [128, 4, TT])
            mean_sb = sb(st, "mean_sb", [128, TT]); m2 = sb(st, "m2", [128, TT]); rstd = sb(st, "rstd", [128, TT])
            pc = sb(st, "pc", [128, 8, TT]); mgb = sb(st, "mgb", [128, 8, TT], BF16)
            sgl = [sb(st, "sgl%d" % i, [128, TT]) for i in range(2)]
            tml = [sb(st, "tml%d" % i, [128, TT]) for i in range(2)]
            yps = [ps(st, "yps%d" % i, [128, 4, 128]) for i in range(2)]
            gaps = [ps(st, "gaps%d" % i, [128, 2, TT]) for i in range(2)]
            stps = ps(st, "stps", [128, 2, TT])
            pwps = ps(st, "pwps", [128, 2, TT])
            wops = ps(st, "wops", [128, 512])
            tpy = ps(st, "tpy", [128, 8, KS], BF16)
            tiles = [(r, mi) for mi, r in enumerate(cfg.main_rows)]

            def issue(i):
                xrow, mi = tiles[i]
                b = i % 2
                for s in range(NS):
                    load(xts[b][:, s, :], h1s[xrow + s * 128: xrow + (s + 1) * 128, :], "xld%d" % b, [(("xt", b), s)])
                for n, src in enumerate((Us, Sfs, Sbs)):
                    load(Ut[b][:, n, :, :].rearrange("p g k -> p (g k)"), src[mi], "ld%d" % b, [(("Ut", b), n)])
                load(gtt[b][:, :, :], gs[:, xrow:xrow + TT].rearrange("(o p) t -> p o t", p=128), "ld%d" % b, [("gtt", b)])
                c0 = cfg.vcol(xrow)
                load(vw[b][:, :, :], vs[:, c0 - 15:c0 + TT + 15].rearrange("(c p) t -> p c t", p=128), "ld%d" % b, [("vw", b)])

            issue(0)
            for i, (xrow, mi) in enumerate(tiles):
                if i + 1 < len(tiles):
                    issue(i + 1)
                b = i % 2
                xt = xts[b]; xkey = ("xt", b); ut = Ut[b]; gt_ = gtt[b]; vwin = vw[b]
                if CUT == 10:
                    for s in range(NS):
                        store(h2s[xrow + s * 128: xrow + (s + 1) * 128, :], xt[:, s, :], "xst%d" % b, [(xkey, s)])
                    continue
                for ct in range(4):
                    S.add("dve", lambda e, ct=ct, vwin=vwin: e.tensor_scalar(out=cc_[:, ct, :], in0=vwin[:, ct, 0:TT], scalar1=convw[:, ct * 31:ct * 31 + 1], scalar2=convb[:, ct:ct + 1],
                                                                              op0=mult, op1=add),
                          reads=[("vw", b), "convw", "convb"], writes=[("cc", ct)])
                    for k in range(1, 31):
                        S.add("dve", lambda e, ct=ct, k=k, vwin=vwin: e.scalar_tensor_tensor(out=cc_[:, ct, :], in0=vwin[:, ct, k:k + TT], scalar=convw[:, ct * 31 + k:ct * 31 + k + 1],
                                                                                               in1=cc_[:, ct, :], op0=mult, op1=add),
                              reads=[("vw", b), "convw"], writes=[("cc", ct)])
                    S.add("act", lambda e, ct=ct: e.copy(out=cbf[:, ct, :], in_=cc_[:, ct, :]), reads=[("cc", ct)], writes=[("cbf", ct)])
                    S.add("act", lambda e, ct=ct: e.activation(out=csq[:, ct, :], in_=cc_[:, ct, :], func=AF.Square), reads=[("cc", ct)], writes=[("csq", ct)])
                if CUT == 11:
                    continue
                for ct in range(4):
                    S.add("pe", lambda e, ct=ct: e.matmul(stps[:, 0, :], lhsT=onesD[:, :], rhs=cbf[:, ct, :], start=(ct == 0), stop=(ct == 3)),
                          reads=[("cbf", ct), "onesD"], writes=["stps"])
                for ct in range(4):
                    S.add("pe", lambda e, ct=ct: e.matmul(stps[:, 1, :], lhsT=onesD[:, :], rhs=csq[:, ct, :], start=(ct == 0), stop=(ct == 3)),
                          reads=[("csq", ct), "onesD"], writes=["stps"])
                S.add("act", lambda e: e.copy(out=mean_sb[:, :], in_=stps[:, 0, :]), reads=["stps"], writes=["mean_sb"])
                if CUT == 13:
                    continue
                S.add("dve", lambda e: e.tensor_tensor(out=m2[:, :], in0=mean_sb[:, :], in1=mean_sb[:, :], op=mult), reads=["mean_sb"], writes=["m2"])
                S.add("dve", lambda e: e.tensor_tensor(out=m2[:, :], in0=stps[:, 1, :], in1=m2[:, :], op=sub), reads=["stps", "m2"], writes=["m2"])
                S.add("dve", lambda e: e.tensor_scalar(out=m2[:, :], in0=m2[:, :], scalar1=0.0, scalar2=EPS, op0=ALU.max, op1=add), reads=["m2"], writes=["m2"])
                S.add("act", lambda e: e.activation(out=rstd[:, :], in_=m2[:, :], func=AF.Sqrt), reads=["m2"], writes=["rstd0"])
                S.add("dve", lambda e: e.reciprocal(out=rstd[:, :], in_=rstd[:, :]), reads=["rstd0"], writes=["rstd"])
                if CUT == 12:
                    continue
                for ct in range(4):
                    S.add("dve", lambda e, ct=ct: e.tensor_tensor(out=ctm[:, ct, :], in0=cc_[:, ct, :], in1=mean_sb[:, :], op=sub), reads=[("cc", ct), "mean_sb"], writes=[("ctm", ct)])
                    S.add("dve", lambda e, ct=ct: e.tensor_tensor(out=ctm[:, ct, :], in0=ctm[:, ct, :], in1=rstd[:, :], op=mult), reads=[("ctm", ct), "rstd"], writes=[("ctm", ct)])
                    S.add("act", lambda e, ct=ct: e.activation(out=cact[:, ct, :], in_=ctm[:, ct, :], func=AF.Silu, scale=lng[:, ct:ct + 1], bias=lnb[:, ct:ct + 1]),
                          reads=[("ctm", ct), "lng", "lnb"], writes=[("cact", ct)])
                for dp in range(4):
                    for hf in range(2):
                        dt_i = dp * 2 + hf
                        for ct in range(4):
                            S.add("pe", lambda e, hf=hf, dt_i=dt_i, ct=ct: e.matmul(pwps[:, hf, :], lhsT=wpw[:, ct, dt_i * 128:(dt_i + 1) * 128], rhs=cact[:, ct, :],
                                                                                    start=(ct == 0), stop=(ct == 3)),
                                  reads=[("cact", ct), ("wpw", ct)], writes=["pwps"])
                    for hf in range(2):
                        dt_i = dp * 2 + hf
                        S.add("dve", lambda e, hf=hf, dt_i=dt_i, gt_=gt_: e.tensor_tensor(out=pc[:, dt_i, :], in0=pwps[:, hf, :], in1=gt_[:, dt_i, :], op=mult),
                              reads=["pwps", ("gtt", b)], writes=[("pc", dt_i)])
                if CUT == 1:
                    continue
                for gb in range(8):
                    yp = yps[gb % 2]; yk = ("yps", gb % 2)
                    for gl in range(4):
                        g = gb * 4 + gl
                        S.add("pe", lambda e, yp=yp, gl=gl, g=g, ut=ut: e.matmul(yp[0:KS, gl, :], lhsT=ut[:, 1, g, :], rhs=CC[0][:, g, :], start=True, stop=False),
                              reads=[(("Ut", b), 1), ("CC", 0)], writes=[yk])
                        S.add("pe", lambda e, yp=yp, gl=gl, g=g, ut=ut: e.matmul(yp[0:KS, gl, :], lhsT=ut[:, 2, g, :], rhs=CC[1][:, g, :], start=False, stop=False),
                              reads=[(("Ut", b), 2), ("CC", 1)], writes=[yk])
                        S.add("pe", lambda e, yp=yp, gl=gl, g=g, ut=ut: e.matmul(yp[0:KS, gl, :], lhsT=ut[:, 0, g, :], rhs=DDt[:, g, :], start=False, stop=True),
                              reads=[(("Ut", b), 0), "DD"], writes=[yk])
                    S.add("act", lambda e, yp=yp, gb=gb: e.copy(out=Ytok[:, :, gb * 64:(gb + 1) * 64].rearrange("k j (g h) -> k j g h", h=16),
                                                                 in_=yp[0:KS, :, :].rearrange("k g (j h) -> k j g h", h=16)),
                          reads=[yk], writes=[("Ytok", gb)])
                for ct in range(4):
                    for j in range(8):
                        S.add("pe", lambda e, ct=ct, j=j: e.transpose(out=tpy[:, j, :], in_=Ytok[:, j, ct * 128:(ct + 1) * 128], identity=ident[0:KS, 0:KS]),
                              reads=[("Ytok", 2 * ct), ("Ytok", 2 * ct + 1), "ident"], writes=["tpy"])
                    S.add("act", lambda e, ct=ct: e.copy(out=yfm[:, ct, :].rearrange("p (k j) -> p k j", j=8), in_=tpy[:, :, :].rearrange("p j k -> p k j")),
                          reads=["tpy"], writes=[("yfm", ct)])
                if CUT == 2:
                    continue
                yk4 = [("yfm", ct) for ct in range(4)]
                F2 = lambda t: t[:, :, :].rearrange("p a b -> p (a b)")
                S.add("dve", lambda e: e.tensor_tensor(out=F2(y2), in0=F2(yfm), in1=F2(yfm), op=mult), reads=yk4, writes=["y2"])
                S.add("dve", lambda e: e.tensor_scalar(out=F2(y2), in0=F2(y2), scalar1=0.044715, scalar2=1.0, op0=mult, op1=add), reads=["y2"], writes=["y2"])
                S.add("dve", lambda e: e.tensor_tensor(out=F2(y2), in0=F2(y2), in1=F2(yfm), op=mult), reads=["y2"] + yk4, writes=["y2"])
                S.add("act", lambda e: e.activation(out=F2(ysg), in_=F2(y2), func=AF.Sigmoid, scale=1.5957691216057308), reads=["y2"], writes=["ysg"])
                S.add("dve", lambda e: e.tensor_tensor(out=F2(ygb), in0=F2(ysg), in1=F2(yfm), op=mult), reads=["ysg"] + yk4, writes=["ygb"])
                for dt_i in range(8):
                    ga = gaps[dt_i % 2]; gk = ("gaps", dt_i % 2)
                    for hf in range(2):
                        c0 = hf * D + dt_i * 128
                        for ct in range(4):
                            S.add("pe", lambda e, ga=ga, hf=hf, c0=c0, ct=ct: e.matmul(ga[:, hf, :], lhsT=wglu[:, ct, c0:c0 + 128], rhs=ygb[:, ct, :], start=(ct == 0), stop=(ct == 3)),
                                  reads=["ygb", ("wglu", ct)], writes=[gk])
                    sg_ = sgl[dt_i % 2]; tm_ = tml[dt_i % 2]
                    S.add("act", lambda e, ga=ga, sg_=sg_: e.activation(out=sg_[:, :], in_=ga[:, 1, :], func=AF.Sigmoid), reads=[gk], writes=[("sgl", dt_i % 2)])
                    S.add("dve", lambda e, ga=ga, sg_=sg_, tm_=tm_: e.tensor_tensor(out=tm_[:, :], in0=ga[:, 0, :], in1=sg_[:, :], op=mult), reads=[gk, ("sgl", dt_i % 2)], writes=[("tml", dt_i % 2)])
                    S.add("dve", lambda e, tm_=tm_, dt_i=dt_i, gt_=gt_: e.tensor_tensor(out=tm_[:, :], in0=tm_[:, :], in1=gt_[:, 8 + dt_i, :], op=mult),
                          reads=[("tml", dt_i % 2), ("gtt", b)], writes=[("tml", dt_i % 2)])
                    S.add("dve", lambda e, tm_=tm_, dt_i=dt_i: e.tensor_tensor(out=mgb[:, dt_i, :], in0=tm_[:, :], in1=pc[:, dt_i, :], op=add),
                          reads=[("tml", dt_i % 2), ("pc", dt_i)], writes=[("mgb", dt_i)])
                if CUT == 3:
                    continue
                for s in range(NS):
                    for half in range(2):
                        for dt_i in range(8):
                            S.add("pe", lambda e, s=s, half=half, dt_i=dt_i: e.matmul(wops[:, :], lhsT=mgb[:, dt_i, s * 128:(s + 1) * 128], rhs=wout[:, dt_i, half * 512:(half + 1) * 512],
                                                                                      start=(dt_i == 0), stop=(dt_i == 7)),
                                  reads=[("mgb", dt_i), ("wout", dt_i)], writes=["wops"])
                        S.add("dve", lambda e, s=s, half=half, xt=xt: e.tensor_tensor(out=xt[:, s, half * 512:(half + 1) * 512], in0=wops[:, :], in1=xt[:, s, half * 512:(half + 1) * 512], op=add),
                              reads=["wops", (xkey, s)], writes=[(xkey, s)])
                    store(h2s[xrow + s * 128: xrow + (s + 1) * 128, :], xt[:, s, :], "xst%d" % b, [(xkey, s)])
            S.flush()

    def p3b_pass():
        with ExitStack() as st:
            ident, gbc = load_consts(st, "x")
            gmem = sb(st, "gmem", [128, D]); load(gmem[:, :], g_d["mem"], "gbc2", ["mem_g"])
            wq = sb(st, "wq", [128, KT, D], BF16); load_w_bf16(wq, wq_d, KT, "wq", D)
            wo = sb(st, "wo", [128, KT, D], BF16); load_w_bf16(wo, wo_d, KT, "wo", D)
            wkv = sb(st, "wkv", [128, KT, 2 * D], BF16); load_w_bf16(wkv, wkv_d, KT, "wkv", 2 * D)
            ones = sb(st, "ones", [128, 128], BF16)
            S.add("dve", lambda e: e.memset(ones[:, :], 1.0), writes=["ones"])
            tmp = norm_tmp(st)
            xts = [sb(st, "xt%d" % i, [128, NS, D]) for i in range(2)]
            ufm = sb(st, "ufm", [128, KT, TT], BF16)
            Kfm = [sb(st, "Kfm%d" % i, [128, KT, 256], BF16) for i in range(2)]
            Vt = [sb(st, "Vt%d" % i, [128, 2, D], BF16) for i in range(2)]
            qsb = sb(st, "qsb", [128, KT, TT], BF16)
            pT = [sb(st, "pT%d" % i, [128, 2, TT], BF16) for i in range(2)]
            rden = [sb(st, "rden%d" % i, [128, TT]) for i in range(2)]
            ofm = sb(st, "ofm", [128, KT, TT], BF16)
            tpp = [ps(st, "tp%d" % i, [128, KT, 128], BF16) for i in range(1)]
            qps = [ps(st, "qps%d" % i, [128, 2, TT]) for i in range(2)]
            sps = [ps(st, "sps%d" % i, [128, 2, TT]) for i in range(1)]
            ops_ = [ps(st, "ops%d" % i, [128, 2, TT]) for i in range(1)]
            dps_ = ps(st, "dps", [128, 512])
            wps = ps(st, "wps", [128, 512])
            for sq in range(2):
                mt = xts[sq]
                for s in range(2):
                    load(mt[:, s, :], mem_d[sq * 256 + s * 128: sq * 256 + (s + 1) * 128, :], "xld%d" % sq, [(("xt", sq), s)])
                norm_T_n(mt, ("xt", sq), gmem, ufm, "ufm", tmp, ident, tpp, "mem", 2)
                ukeys = [("ufm", 0), ("ufm", 1)]
                for dd in range(KT):
                    qp = qps[dd % 2]; qk = ("qps", dd % 2)
                    for kt in range(KT):
                        S.add("pe", lambda e, qp=qp, dd=dd, kt=kt: e.matmul(qp[:, 0, :], lhsT=wkv[:, kt, dd * 128:(dd + 1) * 128], rhs=ufm[:, kt, :], start=(kt == 0), stop=(kt == KT - 1)),
                              reads=ukeys + [("wkv", kt)], writes=[qk])
                    S.add("act", lambda e, qp=qp, dd=dd, sq=sq: e.copy(out=Kfm[sq][:, dd, :], in_=qp[:, 0, :]), reads=[qk], writes=[("Kfm", sq)])
                for kk in range(2):
                    for half in range(2):
                        for kt in range(KT):
                            S.add("pe", lambda e, kk=kk, half=half, kt=kt: e.matmul(wps[:, :], lhsT=ufm[:, kt, kk * 128:(kk + 1) * 128], rhs=wkv[:, kt, D + half * 512:D + (half + 1) * 512],
                                                                                    start=(kt == 0), stop=(kt == KT - 1)),
                                  reads=ukeys + [("wkv", kt)], writes=["wps"])
                        S.add("act", lambda e, kk=kk, half=half, sq=sq: e.copy(out=Vt[sq][:, kk, half * 512:(half + 1) * 512], in_=wps[:, :]), reads=["wps"], writes=[("Vt", sq)])
            tiles = [(r, mi) for mi, r in enumerate(cfg.main_rows)]

            def issue(i):
                xrow, mi = tiles[i]
                for s in range(NS):
                    load(xts[i % 2][:, s, :], h2s[xrow + s * 128: xrow + (s + 1) * 128, :], "xld%d" % (i % 2), [(("xt", i % 2), s)])

            issue(0)
            for i, (xrow, mi) in enumerate(tiles):
                if i + 1 < len(tiles):
                    issue(i + 1)
                b = i % 2
                sq = 0 if mi < cfg.NTP else 1
                xt = xts[b]; xkey = ("xt", b)
                norm_T_n(xt, xkey, gbc, ufm, "ufm", tmp, ident, tpp, "x", NS)
                ukeys = [("ufm", s) for s in range(NS)]
                for dd in range(KT):
                    qp = qps[dd % 2]; qk = ("qps", dd % 2)
                    for kt in range(KT):
                        S.add("pe", lambda e, qp=qp, dd=dd, kt=kt: e.matmul(qp[:, 0, :], lhsT=wq[:, kt, dd * 128:(dd + 1) * 128], rhs=ufm[:, kt, :], start=(kt == 0), stop=(kt == KT - 1)),
                              reads=ukeys + [("wq", kt)], writes=[qk])
                    S.add("act", lambda e, qp=qp, dd=dd: e.copy(out=qsb[:, dd, :], in_=qp[:, 0, :]), reads=[qk], writes=[("qsb", dd)])
                for h in range(4):
                    sp_ = sps[0]; sk = ("sps", 0)
                    ptt = pT[h % 2]; pk = ("pT", h % 2)
                    for kk in range(2):
                        for n, dd in enumerate((2 * h, 2 * h + 1)):
                            S.add("pe", lambda e, sp_=sp_, kk=kk, dd=dd, n=n, sq=sq: e.matmul(sp_[:, kk, :], lhsT=Kfm[sq][:, dd, kk * 128:(kk + 1) * 128], rhs=qsb[:, dd, :], start=(n == 0), stop=(n == 1)),
                                  reads=[("Kfm", sq), ("qsb", dd)], writes=[sk])
                    for kk in range(2):
                        S.add("act", lambda e, sp_=sp_, kk=kk, ptt=ptt: e.activation(out=ptt[:, kk, :], in_=sp_[:, kk, :], func=AF.Exp, scale=1.0 / 16.0),
                              reads=[sk], writes=[(pk, kk)])
                    for kk in range(2):
                        S.add("pe", lambda e, kk=kk, ptt=ptt: e.matmul(dps_[:, 0:TT], lhsT=ones[:, :], rhs=ptt[:, kk, :], start=(kk == 0), stop=(kk == 1)),
                              reads=[(pk, kk), "ones"], writes=["dps"])
                    rd = rden[h % 2]
                    S.add("dve", lambda e, rd=rd: e.reciprocal(out=rd[:, :], in_=dps_[:, 0:TT]), reads=["dps"], writes=[("rden", h % 2)])
                    op_ = ops_[0]
                    for n, dd in enumerate((2 * h, 2 * h + 1)):
                        for kk in range(2):
                            S.add("pe", lambda e, op_=op_, n=n, dd=dd, kk=kk, ptt=ptt, sq=sq: e.matmul(op_[:, n, :], lhsT=Vt[sq][:, kk, dd * 128:(dd + 1) * 128], rhs=ptt[:, kk, :], start=(kk == 0), stop=(kk == 1)),
                                  reads=[("Vt", sq), (pk, kk)], writes=["ops"])
                    for n, dd in enumerate((2 * h, 2 * h + 1)):
                        S.add("dve", lambda e, op_=op_, n=n, dd=dd, rd=rd: e.tensor_tensor(out=ofm[:, dd, :], in0=op_[:, n, :], in1=rd[:, :], op=mult),
                              reads=["ops", ("rden", h % 2)], writes=[("ofm", dd)])
                for s in range(NS):
                    for half in range(2):
                        for dd in range(KT):
                            S.add("pe", lambda e, s=s, half=half, dd=dd: e.matmul(wps[:, :], lhsT=ofm[:, dd, s * 128:(s + 1) * 128], rhs=wo[:, dd, half * 512:(half + 1) * 512],
                                                                                  start=(dd == 0), stop=(dd == KT - 1)),
                                  reads=[("ofm", dd), ("wo", dd)], writes=["wps"])
                        S.add("dve", lambda e, s=s, half=half, xt=xt: e.tensor_tensor(out=xt[:, s, half * 512:(half + 1) * 512], in0=wps[:, :], in1=xt[:, s, half * 512:(half + 1) * 512], op=add),
                              reads=["wps", (xkey, s)], writes=[(xkey, s)])
                    store(h3s[xrow + s * 128: xrow + (s + 1) * 128, :], xt[:, s, :], "xst%d" % b, [(xkey, s)])
            S.flush()

    all_tiles = [(r, TT) for r in cfg.main_rows] + [(r, HALO) for r in cfg.halo_rows]
    main_tiles = [(r, TT) for r in cfg.main_rows]
    if cfg.stop != 10:
        ffn_pass(all_tiles, x_d, lambda r: r, h1s, wgu1_d, wd1_d, "ffn1", False)
    else:
        prep_pass()
        top.close()
        return nc
    if cfg.stop == 1:
        top.close()
        return nc
    p2_pass()
    if cfg.stop == 2:
        top.close()
        return nc
    prep_pass()
    if cfg.stop == 3:
        top.close()
        return nc
    scan_pass()
    if cfg.stop == 4:
        top.close()
        return nc
    p3a_pass()
    if cfg.stop == 5:
        top.close()
        return nc
    p3b_pass()
    if cfg.stop == 6:
        top.close()
        return nc
    ffn_pass(main_tiles, h3s, lambda r: cfg.yrow(r), y_d, wgu2_d, wd2_d, "ffn2", True)
    top.close()
    return nc


def _host_inputs(inp, cfg):
    f = np.float32
    LP, LSC = cfg.LP, cfg.LSC
    LS = LSC * NCORE
    c = lambda a: np.ascontiguousarray(a, dtype=f)
    bc = lambda v: c(np.broadcast_to(np.asarray(v, f).reshape(1, -1), (128, np.asarray(v).size)))
    shared = {
        "wgu1": c(inp["ffn1_wgu"][0]), "wd1": c(inp["ffn1_wd"][0]),
        "wgu2": c(inp["ffn2_wgu"][0]), "wd2": c(inp["ffn2_wd"][0]),
        "win": c(inp["w_in"][0]), "wpw": c(inp["conv_w_pw"][0]), "wglu": c(inp["ssm_w_glu"][0]),
        "wout": c(inp["w_out"][0]), "wq": c(inp["xattn_wq"][0]), "wkv": c(inp["xattn_wkv"][0]), "wo": c(inp["xattn_wo"][0]),
        "g_ffn1": bc(inp["ffn1_g"][0]), "g_mix": bc(inp["mix_g"][0]), "g_x": bc(inp["xattn_g"][0]),
        "g_mem": bc(inp["mem_g"][0]), "g_ffn2": bc(inp["ffn2_g"][0]), "g_fin": bc(inp["final_g"]),
        "binfm": c(np.asarray(inp["b_in"][0]).reshape(28, 128).T),
        "bssm": c(np.asarray(inp["b_in"][0])[1024:1536].reshape(1, 512)),
        "convw": c(np.asarray(inp["conv_w"][0]).reshape(31, 4, 128).transpose(2, 1, 0).reshape(128, 124)),
        "convb": c(np.asarray(inp["conv_b"][0]).reshape(4, 128).T),
        "lng": c(np.asarray(inp["conv_ln_g"][0]).reshape(4, 128).T),
        "lnb": c(np.asarray(inp["conv_ln_b"][0]).reshape(4, 128).T),
        "ident": np.eye(128, dtype=f),
    }

    def ndup(a):
        a = np.asarray(a, f)
        a = np.moveaxis(a, 2, 0)
        a = np.concatenate([a, a], axis=0)
        return c(a.reshape(128, -1))

    shared["lamre"] = ndup(inp["ssm_lam_re"][0])
    shared["lamim"] = ndup(inp["ssm_lam_im"][0])
    shared["logdt"] = c(np.broadcast_to(np.asarray(inp["ssm_log_dt"][0], f).reshape(1, 64), (128, 64)))
    shared["Bre"] = ndup(inp["ssm_b_re"][0])
    shared["Bim"] = ndup(inp["ssm_b_im"][0])
    shared["Cre"] = ndup(np.swapaxes(np.asarray(inp["ssm_c_re"][0]), 2, 3))
    shared["Cim"] = ndup(np.swapaxes(np.asarray(inp["ssm_c_im"][0]), 2, 3))
    dsk = np.asarray(inp["ssm_d"][0], f).reshape(32, 16).T
    shared["dsk"] = c(np.tile(dsk, (8, 1)))
    jj = np.arange(128) // 16
    mf = (jj[:, None] <= jj[None, :]).astype(f)
    mb = (jj[:, None] >= jj[None, :]).astype(f)
    shared["maskf"] = c(np.tile(mf, (1, 4)))
    shared["maskb"] = c(np.tile(mb, (1, 4)))
    shared["maskd"] = np.eye(128, dtype=f)

    xp = np.asarray(inp["x_prompt"], f)
    xs = np.asarray(inp["x_sample"], f)[0]
    mp = np.asarray(inp["mem_prompt"], f)
    ms = np.asarray(inp["mem_sample"], f)[0]
    xs_pad = np.concatenate([np.zeros((HALO, D), f), xs, np.zeros((HALO, D), f)], axis=0)
    maps = []
    for ci in range(NCORE):
        m = dict(shared)
        s0 = ci * LSC
        m["x"] = c(np.concatenate([xp[ci], xs_pad[s0: s0 + LSC + 2 * HALO]], axis=0))
        m["mem"] = c(np.concatenate([mp[ci], ms], axis=0))
        vm = np.ones((128, 2 * HALO), f)
        if ci == 0:
            vm[:, :HALO] = 0.0
        if ci == NCORE - 1:
            vm[:, HALO:] = 0.0
        m["vmask"] = vm
        ohf = np.zeros((128, 7), f)
        ohb = np.zeros((128, 7), f)
        if ci >= 1:
            ohf[:, ci - 1] = 1.0
        if ci <= 6:
            ohb[:, 6 - ci] = 1.0
        m["ohf"] = ohf
        m["ohb"] = ohb
        maps.append(m)
    return maps


def run(inp, LP, LSC, debug=False, stop=99):
    cfg = Cfg(LP, LSC, debug)
    cfg.stop = stop
    nc = build(cfg)
    maps = _host_inputs(inp, cfg)
    res = run_bass_kernel_spmd(nc, maps, core_ids=list(range(NCORE)))
    return res.results, cfg


def kernel(**inputs):
    LP = inputs["x_prompt"].shape[1]
    LSC = inputs["x_sample"].shape[1] // NCORE
    results, cfg = run(inputs, LP, LSC)
    yp = np.stack([np.asarray(results[ci]["y"][:LP], np.float32) for ci in range(NCORE)], axis=0)
    ys = np.concatenate([np.asarray(results[ci]["y"][LP:], np.float32) for ci in range(NCORE)], axis=0)[None]
    return (yp, ys)
```
